# Optimizing a Trainium2 kernel written in Bass

```python
import math
import jax, jax.numpy as jnp
from jax import lax
import numpy as np

D_MODEL = 1024
BATCH = 8
SEQ = 4096
DEPTH = 4

N_A_LAYERS = DEPTH // 2
N_B_LAYERS = DEPTH - N_A_LAYERS
D_FF = 2816
GMLP_EXPAND = 6
D_GMLP = GMLP_EXPAND * D_MODEL
D_GATE = D_GMLP // 2
SGU_GROUPS = 8
SGU_GROUP_DIM = D_GATE // SGU_GROUPS
CHUNK = 128
N_HEADS = 8
HEAD_DIM = D_MODEL // N_HEADS // 2
V_DIM = 2 * HEAD_DIM
Q_WIDTH = N_HEADS * 2 * HEAD_DIM
K_WIDTH = N_HEADS * 2 * HEAD_DIM
KV_WIDTH = K_WIDTH + N_HEADS * V_DIM
Q_BLOCK = 128
RMS_EPS = 1e-6
LN_EPS = 1e-5

kernel_name = "yoco_gmlp_diffattn_macaron_trunk"


def rmsnorm(x, g):
    xf = x.astype(jnp.float32)
    y = xf * lax.rsqrt(jnp.mean(xf * xf, axis=-1, keepdims=True) + RMS_EPS)
    return (y * g.astype(jnp.float32)).astype(x.dtype)


def layernorm(x, g, b):
    xf = x.astype(jnp.float32)
    mu = jnp.mean(xf, axis=-1, keepdims=True)
    var = jnp.mean(jnp.square(xf - mu), axis=-1, keepdims=True)
    y = (xf - mu) * lax.rsqrt(var + LN_EPS)
    return (y * g.astype(jnp.float32) + b.astype(jnp.float32)).astype(x.dtype)


def swiglu_ffn(x, w_gate_up, w_down):
    gate, up = jnp.split(x @ w_gate_up, 2, axis=-1)
    return (jax.nn.silu(gate) * up) @ w_down


def chunked_sgu_mixer(h, w_in, b_in, ln_g, ln_b, w_s, b_s, w_out, b_out):
    B, S, _ = h.shape
    z = jax.nn.gelu(h @ w_in + b_in, approximate=False)
    u, v = jnp.split(z, 2, axis=-1)
    v = layernorm(v, ln_g, ln_b)
    v = v.reshape(B, S // CHUNK, CHUNK, SGU_GROUPS, SGU_GROUP_DIM)
    causal = jnp.tril(jnp.ones((CHUNK, CHUNK), dtype=bool))
    ws = jnp.where(causal[None], w_s, jnp.zeros((), w_s.dtype))
    s = jnp.einsum('gts,bnsgc->bntgc', ws, v) + b_s.T[:, :, None]
    s = s.reshape(B, S, D_GATE)
    return (u * s) @ w_out + b_out


def shared_kv(h_kv, w_kv):
    B, S, _ = h_kv.shape
    kv = h_kv @ w_kv
    k = kv[..., :K_WIDTH].reshape(B, S, N_HEADS, 2, HEAD_DIM)
    v = kv[..., K_WIDTH:].reshape(B, S, N_HEADS, V_DIM)
    return k, v


def diff_attention(h, k, v, w_q, lam_params, subln_g, w_o, lambda_init):
    B, S, _ = h.shape
    n_blocks = S // Q_BLOCK
    q = (h @ w_q).reshape(B, n_blocks, Q_BLOCK, N_HEADS, 2, HEAD_DIM)
    q = jnp.moveaxis(q, 1, 0)
    lp = lam_params.astype(jnp.float32)
    lam = (jnp.exp(jnp.sum(lp[0] * lp[1])) - jnp.exp(jnp.sum(lp[2] * lp[3]))
           + lambda_init)
    scale = HEAD_DIM ** -0.5
    kpos = jnp.arange(S)
    neg = jnp.finfo(jnp.float32).min

    def one_block(args):
        qb, bi = args
        scores = jnp.einsum('bqhcd,bkhcd->bhcqk', qb, k).astype(jnp.float32) * scale
        qpos = bi * Q_BLOCK + jnp.arange(Q_BLOCK)
        mask = kpos[None, :] <= qpos[:, None]
        p = jax.nn.softmax(jnp.where(mask, scores, neg), axis=-1)
        a = p[:, :, 0] - lam * p[:, :, 1]
        return jnp.einsum('bhqk,bkhe->bqhe', a.astype(v.dtype), v)

    o = lax.map(one_block, (q, jnp.arange(n_blocks)))
    o = jnp.moveaxis(o, 0, 1).reshape(B, S, N_HEADS, V_DIM)
    o = rmsnorm(o, subln_g) * (1.0 - lambda_init)
    return o.reshape(B, S, N_HEADS * V_DIM) @ w_o


def setup_inputs(seed: int = 0) -> dict:
    key = jax.random.key(seed)
    ks = jax.random.split(key, 20)
    f32 = jnp.float32

    def nrm(k, shape, scale):
        return jax.random.normal(k, shape, f32) * scale

    return {
        "x": nrm(ks[0], (BATCH, SEQ, D_MODEL), 1.0),
        "norm_g": 1.0 + nrm(ks[1], (DEPTH, 3, D_MODEL), 0.02),
        "ffn_w_gate_up": nrm(ks[2], (DEPTH, 2, D_MODEL, 2 * D_FF), D_MODEL ** -0.5),
        "ffn_w_down": nrm(ks[3], (DEPTH, 2, D_FF, D_MODEL), D_FF ** -0.5),
        "a_w_in": nrm(ks[4], (N_A_LAYERS, D_MODEL, D_GMLP), D_MODEL ** -0.5),
        "a_b_in": nrm(ks[5], (N_A_LAYERS, D_GMLP), 0.02),
        "a_ln_g": 1.0 + nrm(ks[6], (N_A_LAYERS, D_GATE), 0.02),
        "a_ln_b": nrm(ks[7], (N_A_LAYERS, D_GATE), 0.02),
        "a_w_s": nrm(ks[8], (N_A_LAYERS, SGU_GROUPS, CHUNK, CHUNK), CHUNK ** -0.5),
        "a_b_s": 1.0 + nrm(ks[9], (N_A_LAYERS, SGU_GROUPS, CHUNK), 0.02),
        "a_w_out": nrm(ks[10], (N_A_LAYERS, D_GATE, D_MODEL), D_GATE ** -0.5),
        "a_b_out": nrm(ks[11], (N_A_LAYERS, D_MODEL), 0.02),
        "kv_norm_g": 1.0 + nrm(ks[12], (D_MODEL,), 0.02),
        "w_kv": nrm(ks[13], (D_MODEL, KV_WIDTH), D_MODEL ** -0.5),
        "b_w_q": nrm(ks[14], (N_B_LAYERS, D_MODEL, Q_WIDTH), D_MODEL ** -0.5),
        "b_lambda": nrm(ks[15], (N_B_LAYERS, 4, HEAD_DIM), 0.1),
        "b_subln_g": 1.0 + nrm(ks[16], (N_B_LAYERS, V_DIM), 0.02),
        "b_w_o": nrm(ks[17], (N_B_LAYERS, N_HEADS * V_DIM, D_MODEL), (N_HEADS * V_DIM) ** -0.5),
        "final_norm_g": 1.0 + nrm(ks[18], (D_MODEL,), 0.02),
    }


def reference(x, norm_g, ffn_w_gate_up, ffn_w_down, a_w_in, a_b_in, a_ln_g, a_ln_b,
              a_w_s, a_b_s, a_w_out, a_b_out, kv_norm_g, w_kv, b_w_q, b_lambda,
              b_subln_g, b_w_o, final_norm_g):
    h = x
    k_shared = None
    v_shared = None
    for layer in range(DEPTH):
        h = h + 0.5 * swiglu_ffn(rmsnorm(h, norm_g[layer, 0]),
                                 ffn_w_gate_up[layer, 0], ffn_w_down[layer, 0])
        hn = rmsnorm(h, norm_g[layer, 1])
        if layer < N_A_LAYERS:
            i = layer
            h = h + chunked_sgu_mixer(hn, a_w_in[i], a_b_in[i], a_ln_g[i], a_ln_b[i],
                                      a_w_s[i], a_b_s[i], a_w_out[i], a_b_out[i])
        else:
            j = layer - N_A_LAYERS
            lambda_init = 0.8 - 0.6 * math.exp(-0.3 * layer)
            h = h + diff_attention(hn, k_shared, v_shared, b_w_q[j], b_lambda[j],
                                   b_subln_g[j], b_w_o[j], lambda_init)
        h = h + 0.5 * swiglu_ffn(rmsnorm(h, norm_g[layer, 2]),
                                 ffn_w_gate_up[layer, 1], ffn_w_down[layer, 1])
        if layer == N_A_LAYERS - 1:
            k_shared, v_shared = shared_kv(rmsnorm(h, kv_norm_g), w_kv)
    return rmsnorm(h, final_norm_g)
```

```python
import math
from contextlib import ExitStack

import numpy as np
import concourse.bass as bass
import concourse.mybir as mybir
from concourse.bass_utils import run_bass_kernel_spmd

F32 = mybir.dt.float32
BF16 = mybir.dt.bfloat16
ALU = mybir.AluOpType
AF = mybir.ActivationFunctionType

D = 1024
DFF = 2816
DG = 3072
NH = 8
TT = 1024
SUB = 512
NSLOT = 4
SE = 6144
RMS_EPS = 1e-6
LN_EPS = 1e-5
N_A = 2
DEPTH = 4

C_NORM = 0
C_KVN = 96
C_FIN = 104
C_BU = 112
C_LNG = 160
C_LNB = 208
C_BOUT = 256
C_SUB = 272
C_SUBS = 274
C_NLAM = 276
NCOL = 280


class _Op:
    __slots__ = ("emit", "deps", "dma", "needed", "sig", "ring")

    def __init__(self, emit, dma):
        self.emit = emit
        self.deps = []
        self.dma = dma
        self.needed = False
        self.sig = None
        self.ring = None


class Sched:
    ENGS = ("pe", "act", "dve", "pool", "sp")
    KRING = 8

    def __init__(self):
        self.ops = {e: [] for e in self.ENGS}
        self.last_w = {}
        self.readers = {}
        self.dma_ops = {e: [] for e in self.ENGS}

    def add(self, eng, emit, reads=(), writes=(), dma=False):
        lst = self.ops[eng]
        idx = len(lst)
        me = (eng, idx)
        op = _Op(emit, dma)
        wdeps = set()
        rdeps = set()
        for k in reads:
            w = self.last_w.get(k)
            if w is not None:
                wdeps.add(w)
        for k in writes:
            w = self.last_w.get(k)
            if w is not None:
                wdeps.add(w)
            r = self.readers.get(k)
            if r is not None:
                for e, i in r[0].items():
                    rdeps.add((e, i))
                rdeps.update(r[1])
        deps = set()
        for (e, i) in wdeps:
            if e == eng and not self.ops[e][i].dma and eng == "pe":
                continue
            deps.add((e, i))
        for (e, i) in rdeps:
            if e == eng and not self.ops[e][i].dma and eng == "pe":
                continue
            deps.add((e, i))
        if dma:
            j = len(self.dma_ops[eng])
            op.ring = (j % self.KRING, 16 * (j // self.KRING + 1))
            if j >= self.KRING:
                deps.add((eng, self.dma_ops[eng][j - self.KRING]))
            self.dma_ops[eng].append(idx)
        deps.discard(me)
        op.deps = list(deps)
        lst.append(op)
        for k in reads:
            r = self.readers.get(k)
            if r is None:
                r = ({}, [])
                self.readers[k] = r
            if dma:
                r[1].append(me)
            else:
                r[0][eng] = idx
        for k in writes:
            self.last_w[k] = me
            self.readers[k] = ({}, [])
        return me

    def resolve(self):
        for e in self.ENGS:
            for op in self.ops[e]:
                for (de, di) in op.deps:
                    self.ops[de][di].needed = True
        for e in self.ENGS:
            c = 0
            for op in self.ops[e]:
                if op.dma:
                    op.sig = (("ring", e, op.ring[0]), op.ring[1])
                elif op.needed:
                    c += 1
                    op.sig = (("eng", e), c)

    def run_engine(self, eng, engine_obj, sems, final_wait=False):
        waited = {}
        for op in self.ops[eng]:
            need = {}
            for (de, di) in op.deps:
                s, v = self.ops[de][di].sig
                if need.get(s, 0) < v:
                    need[s] = v
            for s, v in need.items():
                if waited.get(s, 0) < v:
                    engine_obj.wait_ge(sems[s], v)
                    waited[s] = v
            ins = op.emit(engine_obj)
            if op.dma:
                ins.then_inc(sems[("ring", eng, op.ring[0])], 16)
            elif op.needed:
                ins.then_inc(sems[("eng", eng)], 1)
        if final_wait:
            final = {}
            for idx in self.dma_ops[eng]:
                op = self.ops[eng][idx]
                final[op.ring[0]] = max(final.get(op.ring[0], 0), op.ring[1])
            for r, v in final.items():
                engine_obj.wait_ge(sems[("ring", eng, r)], v)


class Ring:
    def __init__(self, tensor, name, n, width):
        self.t = tensor
        self.name = name
        self.n = n
        self.w = width
        self.i = 0

    def get(self, width=None):
        i = self.i
        self.i = (i + 1) % self.n
        w = self.w if width is None else width
        return self.t[:, i * self.w:i * self.w + w], (self.name, i)


def default_plan():
    plan = []
    for l in range(DEPTH):
        plan.append(("ffn", l, 0))
        if l < N_A:
            plan.append(("amix", l))
        else:
            plan.append(("attn", l - N_A, l))
        plan.append(("ffn", l, 1))
        if l == N_A - 1:
            plan.append(("kv",))
    plan.append(("final",))
    return plan


def build_program(S=4096, plan=None):
    if plan is None:
        plan = default_plan()
    NT = S // TT
    NCH = S // 128
    has_final = any(st[0] == "final" for st in plan)

    nc = bass.Bass("TRN2", target_bir_lowering=False)

    def din(name, shape, dt=F32):
        return nc.dram_tensor(name, shape, dt, kind="ExternalInput").ap()

    xT = din("xT", [D, S])
    gcol_d = din("gcol", [128, NCOL])
    tri_d = din("tri", [128, 128])
    wgu = din("wgu", [DEPTH * 2 * D, 2 * DFF])
    wdn = din("wdn", [DEPTH * 2 * DFF, D])
    win = din("win", [N_A * D, 2 * DG])
    bin_d = din("bin", [N_A, 2 * DG])
    wsT_d = din("wsT", [N_A * 128, 1024])
    bsrep_d = din("bsrep", [N_A * 128, 1024])
    wout = din("wout", [N_A * DG, D])
    wkv = din("wkv", [D, 2 * D])
    wq = din("wq", [2 * D, D])
    wo = din("wo", [2 * D, D])
    lamrep_d = din("lamrep", [128, 2 * 4 * 64])
    outT = nc.dram_tensor("outT", [D, S], F32, kind="ExternalOutput").ap()
    kT_d = nc.dram_tensor("kT_scr", [D, S], BF16, kind="Internal").ap()
    vS_d = nc.dram_tensor("vS_scr", [NH * 128, NCH * 128], BF16, kind="Internal").ap()

    sc = Sched()
    es = ExitStack()

    def sb(name, n, dt):
        return es.enter_context(nc.sbuf_tensor(name, [128, n], dt))

    with es:
        Ht = sb("Ht", 8 * TT, F32)
        XNt = sb("XNt", 8 * TT, BF16)
        BIGt = sb("BIGt", 24576, BF16)
        SMt = sb("SMt", 6 * 512, F32)
        RSt = sb("RSt", 2 * 1024, F32)
        OCt = sb("OCt", 2 * 1024, F32)
        SQt = sb("SQt", 4 * 512, BF16)
        PTt = sb("PTt", 6 * 512, BF16)
        WSt = sb("WSt", NSLOT * SE, BF16)
        B2t = sb("B2t", 24 * 128, F32)
        BSt = sb("BSt", 1024, F32)
        WSTt = sb("WSTt", 2 * 1024, BF16)
        GC = sb("GC", NCOL, F32)
        STt = sb("STt", 8 * 64, F32)
        ONESD = sb("ONESD", 128, BF16)
        ONESV = sb("ONESV", 128, BF16)
        ONES1 = sb("ONES1", 128, BF16)
        TRI = sb("TRI", 128, BF16)
        EPSC = sb("EPSC", 2, F32)
        PSALL = es.enter_context(nc.psum_tensor("psall", [128, 8 * 512], F32))

        sem_names = [("eng", e) for e in Sched.ENGS]
        for q in ("pool", "sp"):
            for r in range(Sched.KRING):
                sem_names.append(("ring", q, r))
        sems = {}
        for sn in sem_names:
            sems[sn] = es.enter_context(nc.semaphore("_".join(str(x) for x in sn)))

        SM = Ring(SMt, "SM", 6, 512)
        RS = Ring(RSt, "RS", 2, 1024)
        OC = Ring(OCt, "OC", 2, 1024)
        SQ = Ring(SQt, "SQ", 4, 512)
        PT = Ring(PTt, "PT", 3, 1024)
        ST = Ring(STt, "ST", 8, 64)

        ps_state = {"i": 0}

        def psalloc(banks=range(6)):
            banks = list(banks)
            b = banks[ps_state["i"] % len(banks)]
            ps_state["i"] += 1
            return b

        def PS(b, lo=0, n=512):
            return PSALL[:, b * 512 + lo:b * 512 + lo + n]

        def psk(b):
            return ("ps", b)

        def H(kc, sub):
            o = kc * TT + sub * SUB
            return Ht[:, o:o + SUB], ("H", kc, sub)

        def XN(kc, sub):
            o = kc * TT + sub * SUB
            return XNt[:, o:o + SUB], ("XN", kc, sub)

        def BIG(off, n):
            return BIGt[:, off:off + n], [("BIG", g) for g in range(off // 512, (off + n - 1) // 512 + 1)]

        def gc(c):
            return GC[:, c:c + 1]

        CK = ("C",)
        GK = ("GC",)
        add = sc.add

        class WStream:
            def __init__(self):
                self.blocks = []
                self.loaded = 0
                self.cur = 0

            def declare(self, pieces):
                self.blocks.append(pieces)

            def _issue(self, i):
                s = i % NSLOT
                pcs = self.blocks[i]
                for pi, (dstf, src) in enumerate(pcs):
                    wr = [("WS", s, pi)]
                    if pi == len(pcs) - 1:
                        wr += [("WS", s, q) for q in range(pi + 1, 3)]
                    d = dstf(s * SE)
                    add("pool", (lambda g, d=d, src=src: g.dma_start(out=d, in_=src)),
                        reads=[], writes=wr, dma=True)

            def acquire(self):
                while self.loaded < min(len(self.blocks), self.cur + NSLOT):
                    self._issue(self.loaded)
                    self.loaded += 1
                i = self.cur
                self.cur += 1
                s = i % NSLOT
                return s * SE, [("WS", s, 0), ("WS", s, 1), ("WS", s, 2)]

        wstr = WStream()

        def wview(base, n, pat, **kw):
            return WSt[:, base:base + n].rearrange(pat, **kw)

        def decl_ffn(l, j):
            r0 = (l * 2 + j) * D
            for b in range(11):
                c0 = b * 256
                wstr.declare([
                    (lambda base: wview(base, 4096, "p (k c) -> p k c", k=8)[:, :, 0:256],
                     wgu[r0:r0 + D, c0:c0 + 256].rearrange("(k p) c -> p k c", p=128)),
                    (lambda base: wview(base, 4096, "p (k c) -> p k c", k=8)[:, :, 256:512],
                     wgu[r0:r0 + D, DFF + c0:DFF + c0 + 256].rearrange("(k p) c -> p k c", p=128)),
                ])
            r1 = (l * 2 + j) * DFF
            for b in range(4):
                c0 = b * 256
                wstr.declare([
                    (lambda base: wview(base, 11 * 256, "p (f c) -> p f c", f=11),
                     wdn[r1:r1 + 11 * 128, c0:c0 + 256].rearrange("(f p) c -> p f c", p=128)),
                    (lambda base: wview(base + 11 * 256, 11 * 256, "p (f c) -> p f c", f=11),
                     wdn[r1 + 11 * 128:r1 + 22 * 128, c0:c0 + 256].rearrange("(f p) c -> p f c", p=128)),
                ])

        def decl_amix(l):
            r0 = l * D
            for sub in range(2):
                for vb in range(6):
                    c0 = DG + vb * 512
                    wstr.declare([
                        (lambda base: wview(base, 4096, "p (k c) -> p k c", k=8),
                         win[r0:r0 + D, c0:c0 + 512].rearrange("(k p) c -> p k c", p=128)),
                        (lambda base: WSt[0:1, base + 4096:base + 4608],
                         bin_d[l:l + 1, c0:c0 + 512]),
                    ])
                for ub in range(6):
                    c0 = ub * 512
                    wstr.declare([
                        (lambda base: wview(base, 4096, "p (k c) -> p k c", k=8),
                         win[r0:r0 + D, c0:c0 + 512].rearrange("(k p) c -> p k c", p=128)),
                    ])
                r1 = l * DG
                for b in range(4):
                    c0 = b * 256
                    wstr.declare([
                        (lambda base: wview(base, 12 * 256, "p (f c) -> p f c", f=12),
                         wout[r1:r1 + 12 * 128, c0:c0 + 256].rearrange("(f p) c -> p f c", p=128)),
                        (lambda base: wview(base + 12 * 256, 12 * 256, "p (f c) -> p f c", f=12),
                         wout[r1 + 12 * 128:r1 + 24 * 128, c0:c0 + 256].rearrange("(f p) c -> p f c", p=128)),
                    ])

        def decl_kv():
            for b in range(4):
                c0 = b * 512
                wstr.declare([
                    (lambda base: wview(base, 4096, "p (k c) -> p k c", k=8),
                     wkv[:, c0:c0 + 512].rearrange("(k p) c -> p k c", p=128)),
                ])

        def decl_attn(j):
            r0 = j * D
            for b in range(2):
                c0 = b * 512
                wstr.declare([
                    (lambda base: wview(base, 4096, "p (k c) -> p k c", k=8),
                     wq[r0:r0 + D, c0:c0 + 512].rearrange("(k p) c -> p k c", p=128)),
                ])
            for b in range(2):
                c0 = b * 512
                wstr.declare([
                    (lambda base: wview(base, 4096, "p (k c) -> p k c", k=8),
                     wo[r0:r0 + D, c0:c0 + 512].rearrange("(k p) c -> p k c", p=128)),
                ])

        for t in range(NT):
            for st in plan:
                if st[0] == "ffn":
                    decl_ffn(st[1], st[2])
                elif st[0] == "amix":
                    decl_amix(st[1])
                elif st[0] == "kv":
                    decl_kv()
                elif st[0] == "attn":
                    decl_attn(st[1])

        def mm(out, lhsT, rhs, start, stop, reads, writes):
            add("pe", (lambda t: t.matmul(out, lhsT, rhs, start=start, stop=stop)), reads, writes)

        def tt(eng, out, in0, in1, op, reads, writes):
            add(eng, (lambda e: e.tensor_tensor(out=out, in0=in0, in1=in1, op=op)), reads, writes)

        def ts(eng, out, in0, s1, s2, op0, op1, reads, writes):
            if s2 is None:
                add(eng, (lambda e: e.tensor_single_scalar(out=out, in_=in0, scalar=s1, op=op0)), reads, writes)
            else:
                add(eng, (lambda e: e.tensor_scalar(out=out, in0=in0, scalar1=s1, scalar2=s2, op0=op0, op1=op1)),
                    reads, writes)

        def rsqrt_act(out, in_, eps_ap, reads, writes):
            actf(out, in_, AF.Ln, reads, writes, bias=eps_ap)
            actf(out, out, AF.Exp, writes, writes, scale=-0.5)

        def stt(eng, out, in0, scalar, in1, op0, op1, reads, writes):
            add(eng, (lambda e: e.scalar_tensor_tensor(out=out, in0=in0, scalar=scalar, in1=in1,
                                                       op0=op0, op1=op1)), reads, writes)

        def actf(out, in_, func, reads, writes, bias=None, scale=None):
            kw = {}
            if bias is not None:
                kw["bias"] = bias
            if scale is not None:
                kw["scale"] = scale
            add("act", (lambda e: e.activation(out=out, in_=in_, func=func, **kw)), reads, writes)

        add("dve", lambda e: e.memset(ONESD[:, :], 1.0 / D), [], [CK])
        add("dve", lambda e: e.memset(ONESV[:, :], 1.0 / 128), [], [CK])
        add("dve", lambda e: e.memset(ONES1[:, :], 1.0), [], [CK])
        add("dve", lambda e: e.memset(EPSC[:, 0:1], RMS_EPS), [], [CK])
        add("dve", lambda e: e.memset(EPSC[:, 1:2], LN_EPS), [], [CK])
        add("pool", lambda g: g.dma_start(out=TRI[:, :], in_=tri_d), [], [CK], dma=True)
        add("sp", lambda g: g.dma_start(out=GC[:, :], in_=gcol_d), [], [GK], dma=True)
        for l in range(N_A):
            for hf in range(2):
                tmp, tk = SM.get()
                add("sp", (lambda g, tmp=tmp, l=l, hf=hf: g.dma_start(
                    out=tmp, in_=wsT_d[l * 128:(l + 1) * 128, hf * 512:(hf + 1) * 512])), [], [tk], dma=True)
                for g4 in range(4):
                    o = l * 1024 + hf * 512 + g4 * 128
                    tt("dve", WSTt[:, o:o + 128], tmp[:, g4 * 128:(g4 + 1) * 128], TRI[:, :], ALU.mult,
                       [tk, CK], [("WST", l)])
        lamt, lk = SM.get()
        add("sp", lambda g: g.dma_start(out=lamt, in_=lamrep_d), [], [lk], dma=True)
        prod, pk = SM.get()
        for j in range(2):
            layer = N_A + j
            lam_init = 0.8 - 0.6 * math.exp(-0.3 * layer)
            for c in range(2):
                a0 = (j * 4 + 2 * c) * 64
                tt("dve", prod[:, 0:64], lamt[:, a0:a0 + 64], lamt[:, a0 + 64:a0 + 128], ALU.mult, [lk], [pk])
                sc_ap = STt[:, 8 * 64 - 8 + c:8 * 64 - 8 + c + 1]
                add("dve", (lambda e, sc_ap=sc_ap: e.reduce_sum(out=sc_ap, in_=prod[:, 0:64],
                                                                axis=mybir.AxisListType.X)), [pk], [("LS", c)])
                actf(sc_ap, sc_ap, AF.Exp, [("LS", c)], [("LS", c)])
            e1 = STt[:, 8 * 64 - 8:8 * 64 - 7]
            e2 = STt[:, 8 * 64 - 7:8 * 64 - 6]
            stt("dve", gc(C_NLAM + j), e2, -lam_init, e1, ALU.add, ALU.subtract,
                [("LS", 0), ("LS", 1), GK], [GK])
            ts("dve", gc(C_SUBS + j), gc(C_SUB + j), 1.0 - lam_init, None, ALU.mult, None, [GK], [GK])

        STATB = (6, 7)
        prestat = {"pend": None, "cnt": [0, 0]}

        def flush_stat():
            p = prestat["pend"]
            if p is not None:
                sq, sk, sub = p
                c = prestat["cnt"][sub]
                mm(PS(STATB[sub]), ONESD[:, :], sq, c == 0, c == 7, [sk, CK], [psk(STATB[sub])])
                prestat["cnt"][sub] = c + 1
                prestat["pend"] = None

        def h_updated(dc, sub):
            flush_stat()
            h, hk = H(dc, sub)
            sq, sk = SQ.get()
            tt("dve", sq, h, h, ALU.mult, [hk], [sk])
            prestat["pend"] = (sq, sk, sub)

        def emit_norm(cbase, final_tile=None):
            flush_stat()
            pre = prestat["cnt"] == [8, 8]
            assert pre or prestat["cnt"] == [0, 0]
            prestat["cnt"] = [0, 0]
            banks = []
            for sub in range(2):
                if pre:
                    banks.append(STATB[sub])
                    continue
                b = psalloc()
                banks.append(b)
                eng = "dve"
                for kc in range(8):
                    h, hk = H(kc, sub)
                    sq, sk = SQ.get()
                    tt(eng, sq, h, h, ALU.mult, [hk], [sk])
                    mm(PS(b), ONESD[:, :], sq, kc == 0, kc == 7, [sk, CK], [psk(b)])
            rstds = []
            for sub in range(2):
                rstd, rk = RS.get(width=SUB)
                rsqrt_act(rstd, PS(banks[sub]), EPSC[:, 0:1], [psk(banks[sub]), CK], [rk])
                rstds.append((rstd, rk))
            for sub in range(2):
                rstd, rk = rstds[sub]
                eng = "dve"
                for kc in range(8):
                    h, hk = H(kc, sub)
                    if final_tile is None:
                        x, xk = XN(kc, sub)
                        stt(eng, x, h, gc(cbase + kc), rstd, ALU.mult, ALU.mult, [hk, rk, GK], [xk])
                    else:
                        o, ok = SM.get()
                        stt("dve", o, h, gc(cbase + kc), rstd, ALU.mult, ALU.mult, [hk, rk, GK], [ok])
                        t0 = final_tile * TT + sub * SUB
                        add("sp", (lambda g, o=o, kc=kc, t0=t0: g.dma_start(
                            out=outT[kc * 128:(kc + 1) * 128, t0:t0 + SUB], in_=o)), [ok], [], dma=True)

        def emit_ffn(l, j):
            emit_norm(C_NORM + (l * 3 + (0 if j == 0 else 2)) * 8)
            for b in range(11):
                base, wk = wstr.acquire()
                for fc in range(2):
                    f = b * 2 + fc
                    for sub in range(2):
                        pg = psalloc()
                        pu = psalloc()
                        for k in range(8):
                            x, xk = XN(k, sub)
                            o = base + k * 512 + fc * 128
                            mm(PS(pg), WSt[:, o:o + 128], x, k == 0, k == 7, wk + [xk], [psk(pg)])
                        for k in range(8):
                            x, xk = XN(k, sub)
                            o = base + k * 512 + 256 + fc * 128
                            mm(PS(pu), WSt[:, o:o + 128], x, k == 0, k == 7, wk + [xk], [psk(pu)])
                        sg, sgk = SM.get()
                        actf(sg, PS(pg), AF.Silu, [psk(pg)], [sgk])
                        a, ak = BIG(f * TT + sub * SUB, SUB)
                        tt("dve", a, sg, PS(pu), ALU.mult, [sgk, psk(pu)], ak)
            for b in range(4):
                base, wk = wstr.acquire()
                for dc2 in range(2):
                    dc = b * 2 + dc2
                    for sub in range(2):
                        p = psalloc()
                        for f in range(22):
                            a, ak = BIG(f * TT + sub * SUB, SUB)
                            o = base + f * 256 + dc2 * 128
                            mm(PS(p), WSt[:, o:o + 128], a, f == 0, f == 21, wk + ak, [psk(p)])
                        h, hk = H(dc, sub)
                        stt("dve", h, PS(p), 0.5, h, ALU.mult, ALU.add, [psk(p), hk], [hk])
                        h_updated(dc, sub)

        VH0 = 0
        GT0 = 12288

        def emit_amix(l):
            emit_norm(C_NORM + (l * 3 + 1) * 8)
            add("sp", (lambda g: g.dma_start(out=BSt[:, :], in_=bsrep_d[l * 128:(l + 1) * 128, :])),
                [], [("BS",)], dma=True)
            for hf in range(2):
                b = psalloc()
                o = l * 1024 + hf * 512
                mm(PS(b), ONES1[:, :], WSTt[:, o:o + 512], True, True, [("WST", l), CK], [psk(b)])
                for g4 in range(4):
                    g = hf * 4 + g4
                    for c3 in range(3):
                        cc = g * 3 + c3
                        stt("dve", B2t[:, cc * 128:(cc + 1) * 128], PS(b, g4 * 128, 128), gc(C_LNB + l * 24 + cc),
                            BSt[:, g * 128:(g + 1) * 128], ALU.mult, ALU.add,
                            [psk(b), GK, ("BS",)], [("B2", cc)])
            for sub in range(2):
                stats = []
                for tc in range(4):
                    stats.append(ST.get())
                for vb in range(6):
                    base, wk = wstr.acquire()
                    for tc in range(4):
                        b = psalloc()
                        for k in range(8):
                            o = k * TT + sub * SUB + tc * 128
                            mm(PS(b), XNt[:, o:o + 128], WSt[:, base + k * 512:base + (k + 1) * 512],
                               k == 0, False, wk + [("XN", k, sub)], [psk(b)])
                        mm(PS(b), ONES1[0:1, :], WSt[0:1, base + 4096:base + 4608], False, True,
                           wk + [CK], [psk(b)])
                        v, vk = BIG(VH0 + tc * DG + vb * 512, 512)
                        actf(v, PS(b), AF.Gelu, [psk(b)], vk)
                        sa, sk = stats[tc]
                        add("dve", (lambda e, sa=sa, v=v, vb=vb: e.bn_stats(out=sa[:, vb * 6:(vb + 1) * 6], in_=v)),
                            vk, [sk])
                for tc in range(4):
                    sa, sk = stats[tc]
                    add("dve", (lambda e, sa=sa: e.bn_aggr(out=sa[:, 36:38], in_=sa[:, 0:36])), [sk], [sk])
                    rsqrt_act(sa[:, 38:39], sa[:, 37:38], EPSC[:, 1:2], [sk, CK], [sk])
                    ts("dve", sa[:, 39:40], sa[:, 36:37], sa[:, 38:39], -1.0, ALU.mult, ALU.mult, [sk], [sk])
                    v, vk = BIG(VH0 + tc * DG, DG)
                    ts("dve", v, v, sa[:, 38:39], sa[:, 39:40], ALU.mult, ALU.add, [sk] + vk, vk)
                for ub in range(6):
                    base, wk = wstr.acquire()
                    for c4 in range(4):
                        cc = ub * 4 + c4
                        g = cc // 3
                        pu = psalloc()
                        for k in range(8):
                            x, xk = XN(k, sub)
                            o = base + k * 512 + c4 * 128
                            mm(PS(pu), WSt[:, o:o + 128], x, k == 0, k == 7, wk + [xk], [psk(pu)])
                        u, uk = SQ.get()
                        actf(u, PS(pu), AF.Gelu, [psk(pu), GK], [uk], bias=gc(C_BU + l * 24 + cc))
                        pss = psalloc()
                        for tc in range(4):
                            vv, vk = BIG(VH0 + tc * DG + cc * 128, 128)
                            wo_ = l * 1024 + g * 128
                            mm(PS(pss, tc * 128, 128), vv, WSTt[:, wo_:wo_ + 128], True, True,
                               vk + [("WST", l)], [psk(pss)])
                        tmp, tk = SM.get()
                        b2 = B2t[:, cc * 128:(cc + 1) * 128].unsqueeze(1).broadcast_to([128, 4, 128])
                        stt("dve", tmp.rearrange("p (a t) -> p a t", a=4),
                            PS(pss).rearrange("p (a t) -> p a t", a=4),
                            gc(C_LNG + l * 24 + cc), b2, ALU.mult, ALU.add,
                            [psk(pss), GK, ("B2", cc)], [tk])
                        gt, gk = BIG(GT0 + cc * 512, 512)
                        tt("dve", gt, tmp, u, ALU.mult, [tk, uk], gk)
                for b4 in range(4):
                    base, wk = wstr.acquire()
                    for dc2 in range(2):
                        dc = b4 * 2 + dc2
                        p = psalloc()
                        for c in range(24):
                            gt, gk = BIG(GT0 + c * 512, 512)
                            o = base + c * 256 + dc2 * 128
                            mm(PS(p), WSt[:, o:o + 128], gt, c == 0, c == 23, wk + gk, [psk(p)])
                        h, hk = H(dc, sub)
                        stt("dve", h, PS(p), gc(C_BOUT + l * 8 + dc), h, ALU.add, ALU.add,
                            [psk(p), hk, GK], [hk])
                        h_updated(dc, sub)
                flush_stat()

        def emit_kv(tile):
            emit_norm(C_KVN)
            for b in range(2):
                base, wk = wstr.acquire()
                for c4 in range(4):
                    hd = b * 4 + c4
                    for sub in range(2):
                        p = psalloc()
                        for k in range(8):
                            x, xk = XN(k, sub)
                            o = base + k * 512 + c4 * 128
                            mm(PS(p), WSt[:, o:o + 128], x, k == 0, k == 7, wk + [xk], [psk(p)])
                        stg, sk = BIG((hd * 2 + sub) * 512, 512)
                        add("dve", (lambda e, stg=stg, p=p: e.tensor_copy(out=stg, in_=PS(p))), [psk(p)], sk)
                        t0 = tile * TT + sub * SUB
                        add("sp", (lambda g, stg=stg, hd=hd, t0=t0: g.dma_start(
                            out=kT_d[hd * 128:(hd + 1) * 128, t0:t0 + SUB], in_=stg)),
                            sk, [("kT", tile)], dma=True)
            for b in range(2):
                base, wk = wstr.acquire()
                for tc in range(8):
                    sub = tc // 4
                    p = psalloc()
                    for k in range(8):
                        o = k * TT + tc * 128
                        mm(PS(p), XNt[:, o:o + 128], WSt[:, base + k * 512:base + (k + 1) * 512],
                           k == 0, k == 7, wk + [("XN", k, sub)], [psk(p)])
                    stg, sk = BIG(8192 + (b * 8 + tc) * 512, 512)
                    add("dve", (lambda e, stg=stg, p=p: e.tensor_copy(out=stg, in_=PS(p))), [psk(p)], sk)
                    ch = tile * 8 + tc
                    add("sp", (lambda g, stg=stg, b=b, ch=ch: g.dma_start(
                        out=vS_d[b * 512:(b + 1) * 512, ch * 128:(ch + 1) * 128].rearrange("(h p) e -> p h e", p=128),
                        in_=stg.rearrange("p (h e) -> p h e", h=4))),
                        sk, [("vS", tile)], dma=True)

        QT0 = 0
        KB0 = 8192
        VB0 = 16384
        SBANKS = (0, 1, 2)
        MISC = 3
        ACC = (4, 5, 6, 7)

        def emit_attn(j, l, tile):
            emit_norm(C_NORM + (l * 3 + 1) * 8)
            for b in range(2):
                base, wk = wstr.acquire()
                for c4 in range(4):
                    hd = b * 4 + c4
                    for sub in range(2):
                        p = psalloc(SBANKS + (MISC,))
                        for k in range(8):
                            x, xk = XN(k, sub)
                            o = base + k * 512 + c4 * 128
                            mm(PS(p), WSt[:, o:o + 128], x, k == 0, k == 7, wk + [xk], [psk(p)])
                        q, qk = BIG(QT0 + hd * TT + sub * SUB, SUB)
                        ts("dve", q, PS(p), 0.125, None, ALU.mult, None, [psk(p)], qk)
            kend = (tile + 1) * TT
            nkc = kend // 128
            sstate = {"i": 0}

            def load_head(hd):
                bi = hd % 2
                kb, kk = BIG(KB0 + bi * 4096, kend)
                vb, vk = BIG(VB0 + bi * 4096, kend)
                rk = [("kT", t) for t in range(tile + 1)]
                rv = [("vS", t) for t in range(tile + 1)]
                add("sp", (lambda g: g.dma_start(out=kb, in_=kT_d[hd * 128:(hd + 1) * 128, 0:kend])),
                    rk, kk, dma=True)
                add("sp", (lambda g: g.dma_start(out=vb, in_=vS_d[hd * 128:(hd + 1) * 128, 0:kend])),
                    rv, vk, dma=True)

            load_head(0)
            pending = [None]
            pending1 = [None]
            DEFER_K = 3
            for hd in range(NH):
                if hd + 1 < NH:
                    load_head(hd + 1)
                bi = hd % 2
                for sub in range(2):
                    q0 = tile * TT + sub * SUB
                    nk = (q0 + SUB) // 128
                    pt = {}

                    def emit_S(kc):
                        off = max(0, kc * 128 - q0)
                        n = SUB - off
                        slot = sstate["i"] % 2
                        sstate["i"] += 1
                        ko = KB0 + bi * 4096 + kc * 128
                        qo = QT0 + hd * TT + sub * SUB + off
                        for c in range(2):
                            mm(PS(2 * slot + c, off, n), BIGt[c * 64:(c + 1) * 64, ko:ko + 128],
                               BIGt[c * 64:(c + 1) * 64, qo:qo + n], True, True,
                               BIG(ko, 128)[1] + BIG(qo, n)[1], [psk(2 * slot + c)])
                        p, pk = PT.get()
                        p3 = p.rearrange("p (c n) -> p c n", c=2)
                        s3 = PSALL[:, slot * 1024:(slot + 1) * 1024].rearrange("p (c n) -> p c n", c=2)
                        if off == 0:
                            actf(p, PSALL[:, slot * 1024:(slot + 1) * 1024], AF.Exp,
                                 [psk(2 * slot), psk(2 * slot + 1)], [pk])
                        else:
                            actf(p3[:, :, off:SUB], s3[:, :, off:SUB], AF.Exp,
                                 [psk(2 * slot), psk(2 * slot + 1)], [pk])
                        if kc * 128 >= q0:
                            tt("dve", p3[:, :, off:off + 128], p3[:, :, off:off + 128],
                               TRI[:, :].unsqueeze(1).broadcast_to([128, 2, 128]), ALU.mult, [pk, CK], [pk])
                        pt[kc] = (p, pk, off)

                    def emit_A(kc):
                        p, pk, off = pt.pop(kc)
                        n = SUB - off
                        vo = VB0 + bi * 4096 + kc * 128
                        first = (kc == 0)
                        last = (kc == nk - 1)
                        for c in range(2):
                            mm(PS(ACC[c], off, n), BIGt[:, vo:vo + 128], p[:, c * SUB + off:(c + 1) * SUB],
                               first, last, BIG(vo, 128)[1] + [pk], [psk(ACC[c])])
                        for c in range(2):
                            mm(PS(ACC[2 + c], off, n), ONES1[:, :], p[:, c * SUB + off:(c + 1) * SUB],
                               first, last, [CK, pk], [psk(ACC[2 + c])])

                    LA = 2
                    for kc in range(min(LA, nk)):
                        emit_S(kc)
                    if pending1[0] is not None:
                        pending1[0]()
                        pending1[0] = None
                    for kc in range(nk):
                        emit_A(kc)
                        if kc + LA < nk:
                            emit_S(kc + LA)
                        if pending[0] is not None and kc == min(nk - 1, DEFER_K):
                            pending[0]()
                            pending[0] = None
                    rs2, rs2k = RS.get()
                    actf(rs2, PSALL[:, ACC[2] * 512:(ACC[3] + 1) * 512], AF.Ln,
                         [psk(ACC[2]), psk(ACC[3])], [rs2k])
                    t12, t12k = OC.get()
                    add("dve", (lambda e, t12=t12: e.tensor_copy(
                        out=t12, in_=PSALL[:, ACC[0] * 512:(ACC[1] + 1) * 512])),
                        [psk(ACC[0]), psk(ACC[1])], [t12k])
                    osq, osk = SQ.get()
                    o, okk = t12[:, 0:SUB], t12k

                    def part1b(rs2=rs2, rs2k=rs2k, t12=t12, t12k=t12k, osq=osq, osk=osk):
                        actf(rs2, rs2, AF.Exp, [rs2k], [rs2k], scale=-1.0)
                        tt("dve", t12, t12, rs2, ALU.mult, [t12k, rs2k], [t12k])
                        stt("dve", t12[:, 0:SUB], t12[:, SUB:2 * SUB], gc(C_NLAM + j), t12[:, 0:SUB],
                            ALU.mult, ALU.add, [t12k, GK], [t12k])
                        tt("dve", osq, t12[:, 0:SUB], t12[:, 0:SUB], ALU.mult, [t12k], [osk])

                    pending1[0] = part1b

                    def part2(o=o, okk=okk, osq=osq, osk=osk, hd=hd, sub=sub):
                        slot = sstate["i"] % 2
                        sstate["i"] += 1
                        mm(PS(2 * slot), ONESV[:, :], osq, True, True, [osk, CK], [psk(2 * slot)])
                        rs, rsk = SM.get()
                        rsqrt_act(rs, PS(2 * slot), EPSC[:, 0:1], [psk(2 * slot), CK], [rsk])
                        on, onk = XN(hd, sub)
                        stt("dve", on, o, gc(C_SUBS + j), rs, ALU.mult, ALU.mult, [okk, rsk, GK], [onk])

                    pending[0] = part2
            if pending1[0] is not None:
                pending1[0]()
                pending1[0] = None
            if pending[0] is not None:
                pending[0]()
                pending[0] = None
            r0 = j * D
            for b in range(2):
                base, wk = wstr.acquire()
                for c4 in range(4):
                    dc = b * 4 + c4
                    for sub in range(2):
                        p = psalloc(SBANKS + (MISC,))
                        for hd in range(NH):
                            on, onk = XN(hd, sub)
                            o = base + hd * 512 + c4 * 128
                            mm(PS(p), WSt[:, o:o + 128], on, hd == 0, hd == NH - 1, wk + [onk], [psk(p)])
                        h, hk = H(dc, sub)
                        tt("dve", h, PS(p), h, ALU.add, [psk(p), hk], [hk])
                        h_updated(dc, sub)

        for t in range(NT):
            for kc in range(8):
                for sub in range(2):
                    h, hk = H(kc, sub)
                    t0 = t * TT + sub * SUB
                    add("sp", (lambda g, h=h, kc=kc, t0=t0: g.dma_start(
                        out=h, in_=xT[kc * 128:(kc + 1) * 128, t0:t0 + SUB])), [], [hk], dma=True)
            flush_stat()
            prestat["cnt"] = [0, 0]
            for st in plan:
                if st[0] == "ffn":
                    emit_ffn(st[1], st[2])
                elif st[0] == "amix":
                    emit_amix(st[1])
                elif st[0] == "kv":
                    emit_kv(t)
                elif st[0] == "attn":
                    emit_attn(st[1], st[2], t)
                elif st[0] == "final":
                    emit_norm(C_FIN, final_tile=t)
            if not has_final:
                for kc in range(8):
                    for sub in range(2):
                        h, hk = H(kc, sub)
                        t0 = t * TT + sub * SUB
                        add("sp", (lambda g, h=h, kc=kc, t0=t0: g.dma_start(
                            out=outT[kc * 128:(kc + 1) * 128, t0:t0 + SUB], in_=h)), [hk], [], dma=True)

        sc.resolve()
        with nc.Block() as block:
            @block.tensor
            def _(e):
                sc.run_engine("pe", e, sems)

            @block.scalar
            def _(e):
                sc.run_engine("act", e, sems)

            @block.vector
            def _(e):
                sc.run_engine("dve", e, sems)

            @block.gpsimd
            def _(e):
                sc.run_engine("pool", e, sems, final_wait=True)

            @block.sync
            def _(e):
                sc.run_engine("sp", e, sems, final_wait=True)
    return nc


def make_shared_inputs(norm_g, ffn_w_gate_up, ffn_w_down, a_w_in, a_b_in, a_ln_g, a_ln_b,
                       a_w_s, a_b_s, a_w_out, a_b_out, kv_norm_g, w_kv, b_w_q, b_lambda,
                       b_subln_g, b_w_o, final_norm_g):
    f = np.float32
    gcol = np.zeros((128, NCOL), f)
    gcol[:, C_NORM:C_NORM + 96] = np.asarray(norm_g, f).reshape(12, 8, 128).transpose(2, 0, 1).reshape(128, 96)
    gcol[:, C_KVN:C_KVN + 8] = np.asarray(kv_norm_g, f).reshape(8, 128).T
    gcol[:, C_FIN:C_FIN + 8] = np.asarray(final_norm_g, f).reshape(8, 128).T
    abin = np.asarray(a_b_in, f)
    gcol[:, C_BU:C_BU + 48] = abin[:, :DG].reshape(2, 24, 128).transpose(2, 0, 1).reshape(128, 48)
    gcol[:, C_LNG:C_LNG + 48] = np.asarray(a_ln_g, f).reshape(2, 24, 128).transpose(2, 0, 1).reshape(128, 48)
    gcol[:, C_LNB:C_LNB + 48] = np.asarray(a_ln_b, f).reshape(2, 24, 128).transpose(2, 0, 1).reshape(128, 48)
    gcol[:, C_BOUT:C_BOUT + 16] = np.asarray(a_b_out, f).reshape(2, 8, 128).transpose(2, 0, 1).reshape(128, 16)
    gcol[:, C_SUB:C_SUB + 2] = np.asarray(b_subln_g, f).T
    tri = np.triu(np.ones((128, 128), f))
    ws = np.asarray(a_w_s, f)
    wsT = np.ascontiguousarray(ws.transpose(0, 3, 1, 2)).reshape(N_A * 128, 1024)
    bs = np.asarray(a_b_s, f).reshape(N_A, 1, 1024)
    bsrep = np.ascontiguousarray(np.broadcast_to(bs, (N_A, 128, 1024))).reshape(N_A * 128, 1024)
    lamrep = np.ascontiguousarray(np.broadcast_to(np.asarray(b_lambda, f).reshape(1, 512), (128, 512)))
    return {
        "gcol": gcol,
        "tri": tri,
        "wgu": np.ascontiguousarray(np.asarray(ffn_w_gate_up, f)).reshape(DEPTH * 2 * D, 2 * DFF),
        "wdn": np.ascontiguousarray(np.asarray(ffn_w_down, f)).reshape(DEPTH * 2 * DFF, D),
        "win": np.ascontiguousarray(np.asarray(a_w_in, f)).reshape(N_A * D, 2 * DG),
        "bin": np.ascontiguousarray(abin),
        "wsT": wsT,
        "bsrep": bsrep,
        "wout": np.ascontiguousarray(np.asarray(a_w_out, f)).reshape(N_A * DG, D),
        "wkv": np.ascontiguousarray(np.asarray(w_kv, f)),
        "wq": np.ascontiguousarray(np.asarray(b_w_q, f)).reshape(2 * D, D),
        "wo": np.ascontiguousarray(np.asarray(b_w_o, f)).reshape(2 * D, D),
        "lamrep": lamrep,
    }


_NC_CACHE = {}


def kernel(x, norm_g, ffn_w_gate_up, ffn_w_down, a_w_in, a_b_in, a_ln_g, a_ln_b,
           a_w_s, a_b_s, a_w_out, a_b_out, kv_norm_g, w_kv, b_w_q, b_lambda,
           b_subln_g, b_w_o, final_norm_g):
    x = np.asarray(x, np.float32)
    B, S, _ = x.shape
    shared = make_shared_inputs(norm_g, ffn_w_gate_up, ffn_w_down, a_w_in, a_b_in, a_ln_g, a_ln_b,
                                a_w_s, a_b_s, a_w_out, a_b_out, kv_norm_g, w_kv, b_w_q, b_lambda,
                                b_subln_g, b_w_o, final_norm_g)
    key = ("full", S)
    if key not in _NC_CACHE:
        _NC_CACHE[key] = build_program(S=S)
    nc = _NC_CACHE[key]
    in_maps = []
    for b in range(B):
        m = dict(shared)
        m["xT"] = np.ascontiguousarray(x[b].T)
        in_maps.append(m)
    res = run_bass_kernel_spmd(nc, in_maps, core_ids=list(range(B)))
    out = np.empty((B, S, D), np.float32)
    for b in range(B):
        out[b] = res.results[b]["outT"].T
    return out
```

```python
import math
from contextlib import ExitStack

import numpy as np
import concourse.bass as bass
import concourse.mybir as mybir
from concourse.bass_utils import run_bass_kernel_spmd

F32 = mybir.dt.float32
BF16 = mybir.dt.bfloat16
ALU = mybir.AluOpType
AF = mybir.ActivationFunctionType

D = 1024
DFF = 2816
DG = 3072
NH = 8
TT = 1024
SUB = 512
NSLOT = 4
SE = 6144
RMS_EPS = 1e-6
LN_EPS = 1e-5
N_A = 2
DEPTH = 4

C_NORM = 0
C_KVN = 96
C_FIN = 104
C_BU = 112
C_LNG = 160
C_LNB = 208
C_BOUT = 256
C_SUB = 272
C_SUBS = 274
C_NLAM = 276
NCOL = 280


class _Op:
    __slots__ = ("emit", "deps", "dma", "needed", "sig", "ring")

    def __init__(self, emit, dma):
        self.emit = emit
        self.deps = []
        self.dma = dma
        self.needed = False
        self.sig = None
        self.ring = None


class Sched:
    ENGS = ("pe", "act", "dve", "pool", "sp")
    KRING = 8

    def __init__(self):
        self.ops = {e: [] for e in self.ENGS}
        self.last_w = {}
        self.readers = {}
        self.dma_ops = {e: [] for e in self.ENGS}

    def add(self, eng, emit, reads=(), writes=(), dma=False):
        lst = self.ops[eng]
        idx = len(lst)
        me = (eng, idx)
        op = _Op(emit, dma)
        wdeps = set()
        rdeps = set()
        for k in reads:
            w = self.last_w.get(k)
            if w is not None:
                wdeps.add(w)
        for k in writes:
            w = self.last_w.get(k)
            if w is not None:
                wdeps.add(w)
            r = self.readers.get(k)
            if r is not None:
                for e, i in r[0].items():
                    rdeps.add((e, i))
                rdeps.update(r[1])
        deps = set()
        for (e, i) in wdeps:
            if e == eng and not self.ops[e][i].dma and eng == "pe":
                continue
            deps.add((e, i))
        for (e, i) in rdeps:
            if e == eng and not self.ops[e][i].dma and eng == "pe":
                continue
            deps.add((e, i))
        if dma:
            j = len(self.dma_ops[eng])
            op.ring = (j % self.KRING, 16 * (j // self.KRING + 1))
            if j >= self.KRING:
                deps.add((eng, self.dma_ops[eng][j - self.KRING]))
            self.dma_ops[eng].append(idx)
        deps.discard(me)
        op.deps = list(deps)
        lst.append(op)
        for k in reads:
            r = self.readers.get(k)
            if r is None:
                r = ({}, [])
                self.readers[k] = r
            if dma:
                r[1].append(me)
            else:
                r[0][eng] = idx
        for k in writes:
            self.last_w[k] = me
            self.readers[k] = ({}, [])
        return me

    def resolve(self):
        for e in self.ENGS:
            for op in self.ops[e]:
                for (de, di) in op.deps:
                    self.ops[de][di].needed = True
        for e in self.ENGS:
            c = 0
            for op in self.ops[e]:
                if op.dma:
                    op.sig = (("ring", e, op.ring[0]), op.ring[1])
                elif op.needed:
                    c += 1
                    op.sig = (("eng", e), c)

    def run_engine(self, eng, engine_obj, sems, final_wait=False):
        waited = {}
        for op in self.ops[eng]:
            need = {}
            for (de, di) in op.deps:
                s, v = self.ops[de][di].sig
                if need.get(s, 0) < v:
                    need[s] = v
            for s, v in need.items():
                if waited.get(s, 0) < v:
                    engine_obj.wait_ge(sems[s], v)
                    waited[s] = v
            ins = op.emit(engine_obj)
            if op.dma:
                ins.then_inc(sems[("ring", eng, op.ring[0])], 16)
            elif op.needed:
                ins.then_inc(sems[("eng", eng)], 1)
        if final_wait:
            final = {}
            for idx in self.dma_ops[eng]:
                op = self.ops[eng][idx]
                final[op.ring[0]] = max(final.get(op.ring[0], 0), op.ring[1])
            for r, v in final.items():
                engine_obj.wait_ge(sems[("ring", eng, r)], v)


class Ring:
    def __init__(self, tensor, name, n, width):
        self.t = tensor
        self.name = name
        self.n = n
        self.w = width
        self.i = 0

    def get(self, width=None):
        i = self.i
        self.i = (i + 1) % self.n
        w = self.w if width is None else width
        return self.t[:, i * self.w:i * self.w + w], (self.name, i)


def default_plan():
    plan = []
    for l in range(DEPTH):
        plan.append(("ffn", l, 0))
        if l < N_A:
            plan.append(("amix", l))
        else:
            plan.append(("attn", l - N_A, l))
        plan.append(("ffn", l, 1))
        if l == N_A - 1:
            plan.append(("kv",))
    plan.append(("final",))
    return plan


def build_program(S=4096, plan=None):
    if plan is None:
        plan = default_plan()
    NT = S // TT
    NCH = S // 128
    has_final = any(st[0] == "final" for st in plan)

    nc = bass.Bass("TRN2", target_bir_lowering=False)

    def din(name, shape, dt=F32):
        return nc.dram_tensor(name, shape, dt, kind="ExternalInput").ap()

    xT = din("xT", [D, S])
    gcol_d = din("gcol", [128, NCOL])
    tri_d = din("tri", [128, 128])
    wgu = din("wgu", [DEPTH * 2 * D, 2 * DFF])
    wdn = din("wdn", [DEPTH * 2 * DFF, D])
    win = din("win", [N_A * D, 2 * DG])
    bin_d = din("bin", [N_A, 2 * DG])
    wsT_d = din("wsT", [N_A * 128, 1024])
    bsrep_d = din("bsrep", [N_A * 128, 1024])
    wout = din("wout", [N_A * DG, D])
    wkv = din("wkv", [D, 2 * D])
    wq = din("wq", [2 * D, D])
    wo = din("wo", [2 * D, D])
    lamrep_d = din("lamrep", [128, 2 * 4 * 64])
    outT = nc.dram_tensor("outT", [D, S], F32, kind="ExternalOutput").ap()
    kT_d = nc.dram_tensor("kT_scr", [D, S], BF16, kind="Internal").ap()
    vS_d = nc.dram_tensor("vS_scr", [NH * 128, NCH * 128], BF16, kind="Internal").ap()

    sc = Sched()
    es = ExitStack()

    def sb(name, n, dt):
        return es.enter_context(nc.sbuf_tensor(name, [128, n], dt))

    with es:
        Ht = sb("Ht", 8 * TT, F32)
        XNt = sb("XNt", 8 * TT, BF16)
        BIGt = sb("BIGt", 24576, BF16)
        SMt = sb("SMt", 6 * 512, F32)
        RSt = sb("RSt", 2 * 1024, F32)
        OCt = sb("OCt", 2 * 1024, F32)
        SQt = sb("SQt", 4 * 512, BF16)
        PTt = sb("PTt", 6 * 512, BF16)
        WSt = sb("WSt", NSLOT * SE, BF16)
        B2t = sb("B2t", 24 * 128, F32)
        BSt = sb("BSt", 1024, F32)
        WSTt = sb("WSTt", 2 * 1024, BF16)
        GC = sb("GC", NCOL, F32)
        STt = sb("STt", 8 * 64, F32)
        ONESD = sb("ONESD", 128, BF16)
        ONESV = sb("ONESV", 128, BF16)
        ONES1 = sb("ONES1", 128, BF16)
        TRI = sb("TRI", 128, BF16)
        EPSC = sb("EPSC", 2, F32)
        PSALL = es.enter_context(nc.psum_tensor("psall", [128, 8 * 512], F32))

        sem_names = [("eng", e) for e in Sched.ENGS]
        for q in ("pool", "sp"):
            for r in range(Sched.KRING):
                sem_names.append(("ring", q, r))
        sems = {}
        for sn in sem_names:
            sems[sn] = es.enter_context(nc.semaphore("_".join(str(x) for x in sn)))

        SM = Ring(SMt, "SM", 6, 512)
        RS = Ring(RSt, "RS", 2, 1024)
        OC = Ring(OCt, "OC", 2, 1024)
        SQ = Ring(SQt, "SQ", 4, 512)
        PT = Ring(PTt, "PT", 3, 1024)
        ST = Ring(STt, "ST", 8, 64)

        ps_state = {"i": 0}

        def psalloc(banks=range(6)):
            banks = list(banks)
            b = banks[ps_state["i"] % len(banks)]
            ps_state["i"] += 1
            return b

        def PS(b, lo=0, n=512):
            return PSALL[:, b * 512 + lo:b * 512 + lo + n]

        def psk(b):
            return ("ps", b)

        def H(kc, sub):
            o = kc * TT + sub * SUB
            return Ht[:, o:o + SUB], ("H", kc, sub)

        def XN(kc, sub):
            o = kc * TT + sub * SUB
            return XNt[:, o:o + SUB], ("XN", kc, sub)

        def BIG(off, n):
            return BIGt[:, off:off + n], [("BIG", g) for g in range(off // 512, (off + n - 1) // 512 + 1)]

        def gc(c):
            return GC[:, c:c + 1]

        CK = ("C",)
        GK = ("GC",)
        add = sc.add

        class WStream:
            def __init__(self):
                self.blocks = []
                self.loaded = 0
                self.cur = 0

            def declare(self, pieces):
                self.blocks.append(pieces)

            def _issue(self, i):
                s = i % NSLOT
                pcs = self.blocks[i]
                for pi, (dstf, src) in enumerate(pcs):
                    wr = [("WS", s, pi)]
                    if pi == len(pcs) - 1:
                        wr += [("WS", s, q) for q in range(pi + 1, 3)]
                    d = dstf(s * SE)
                    add("pool", (lambda g, d=d, src=src: g.dma_start(out=d, in_=src)),
                        reads=[], writes=wr, dma=True)

            def acquire(self):
                while self.loaded < min(len(self.blocks), self.cur + NSLOT):
                    self._issue(self.loaded)
                    self.loaded += 1
                i = self.cur
                self.cur += 1
                s = i % NSLOT
                return s * SE, [("WS", s, 0), ("WS", s, 1), ("WS", s, 2)]

        wstr = WStream()

        def wview(base, n, pat, **kw):
            return WSt[:, base:base + n].rearrange(pat, **kw)

        def decl_ffn(l, j):
            r0 = (l * 2 + j) * D
            for b in range(11):
                c0 = b * 256
                wstr.declare([
                    (lambda base: wview(base, 4096, "p (k c) -> p k c", k=8)[:, :, 0:256],
                     wgu[r0:r0 + D, c0:c0 + 256].rearrange("(k p) c -> p k c", p=128)),
                    (lambda base: wview(base, 4096, "p (k c) -> p k c", k=8)[:, :, 256:512],
                     wgu[r0:r0 + D, DFF + c0:DFF + c0 + 256].rearrange("(k p) c -> p k c", p=128)),
                ])
            r1 = (l * 2 + j) * DFF
            for b in range(4):
                c0 = b * 256
                wstr.declare([
                    (lambda base: wview(base, 11 * 256, "p (f c) -> p f c", f=11),
                     wdn[r1:r1 + 11 * 128, c0:c0 + 256].rearrange("(f p) c -> p f c", p=128)),
                    (lambda base: wview(base + 11 * 256, 11 * 256, "p (f c) -> p f c", f=11),
                     wdn[r1 + 11 * 128:r1 + 22 * 128, c0:c0 + 256].rearrange("(f p) c -> p f c", p=128)),
                ])

        def decl_amix(l):
            r0 = l * D
            for sub in range(2):
                for vb in range(6):
                    c0 = DG + vb * 512
                    wstr.declare([
                        (lambda base: wview(base, 4096, "p (k c) -> p k c", k=8),
                         win[r0:r0 + D, c0:c0 + 512].rearrange("(k p) c -> p k c", p=128)),
                        (lambda base: WSt[0:1, base + 4096:base + 4608],
                         bin_d[l:l + 1, c0:c0 + 512]),
                    ])
                for ub in range(6):
                    c0 = ub * 512
                    wstr.declare([
                        (lambda base: wview(base, 4096, "p (k c) -> p k c", k=8),
                         win[r0:r0 + D, c0:c0 + 512].rearrange("(k p) c -> p k c", p=128)),
                    ])
                r1 = l * DG
                for b in range(4):
                    c0 = b * 256
                    wstr.declare([
                        (lambda base: wview(base, 12 * 256, "p (f c) -> p f c", f=12),
                         wout[r1:r1 + 12 * 128, c0:c0 + 256].rearrange("(f p) c -> p f c", p=128)),
                        (lambda base: wview(base + 12 * 256, 12 * 256, "p (f c) -> p f c", f=12),
                         wout[r1 + 12 * 128:r1 + 24 * 128, c0:c0 + 256].rearrange("(f p) c -> p f c", p=128)),
                    ])

        def decl_kv():
            for b in range(4):
                c0 = b * 512
                wstr.declare([
                    (lambda base: wview(base, 4096, "p (k c) -> p k c", k=8),
                     wkv[:, c0:c0 + 512].rearrange("(k p) c -> p k c", p=128)),
                ])

        def decl_attn(j):
            r0 = j * D
            for b in range(2):
                c0 = b * 512
                wstr.declare([
                    (lambda base: wview(base, 4096, "p (k c) -> p k c", k=8),
                     wq[r0:r0 + D, c0:c0 + 512].rearrange("(k p) c -> p k c", p=128)),
                ])
            for b in range(2):
                c0 = b * 512
                wstr.declare([
                    (lambda base: wview(base, 4096, "p (k c) -> p k c", k=8),
                     wo[r0:r0 + D, c0:c0 + 512].rearrange("(k p) c -> p k c", p=128)),
                ])

        for t in range(NT):
            for st in plan:
                if st[0] == "ffn":
                    decl_ffn(st[1], st[2])
                elif st[0] == "amix":
                    decl_amix(st[1])
                elif st[0] == "kv":
                    decl_kv()
                elif st[0] == "attn":
                    decl_attn(st[1])

        def mm(out, lhsT, rhs, start, stop, reads, writes):
            add("pe", (lambda t: t.matmul(out, lhsT, rhs, start=start, stop=stop)), reads, writes)

        def tt(eng, out, in0, in1, op, reads, writes):
            add(eng, (lambda e: e.tensor_tensor(out=out, in0=in0, in1=in1, op=op)), reads, writes)

        def ts(eng, out, in0, s1, s2, op0, op1, reads, writes):
            if s2 is None:
                add(eng, (lambda e: e.tensor_single_scalar(out=out, in_=in0, scalar=s1, op=op0)), reads, writes)
            else:
                add(eng, (lambda e: e.tensor_scalar(out=out, in0=in0, scalar1=s1, scalar2=s2, op0=op0, op1=op1)),
                    reads, writes)

        def rsqrt_act(out, in_, eps_ap, reads, writes):
            actf(out, in_, AF.Ln, reads, writes, bias=eps_ap)
            actf(out, out, AF.Exp, writes, writes, scale=-0.5)

        def stt(eng, out, in0, scalar, in1, op0, op1, reads, writes):
            add(eng, (lambda e: e.scalar_tensor_tensor(out=out, in0=in0, scalar=scalar, in1=in1,
                                                       op0=op0, op1=op1)), reads, writes)

        def actf(out, in_, func, reads, writes, bias=None, scale=None):
            kw = {}
            if bias is not None:
                kw["bias"] = bias
            if scale is not None:
                kw["scale"] = scale
            add("act", (lambda e: e.activation(out=out, in_=in_, func=func, **kw)), reads, writes)

        add("dve", lambda e: e.memset(ONESD[:, :], 1.0 / D), [], [CK])
        add("dve", lambda e: e.memset(ONESV[:, :], 1.0 / 128), [], [CK])
        add("dve", lambda e: e.memset(ONES1[:, :], 1.0), [], [CK])
        add("dve", lambda e: e.memset(EPSC[:, 0:1], RMS_EPS), [], [CK])
        add("dve", lambda e: e.memset(EPSC[:, 1:2], LN_EPS), [], [CK])
        add("pool", lambda g: g.dma_start(out=TRI[:, :], in_=tri_d), [], [CK], dma=True)
        add("sp", lambda g: g.dma_start(out=GC[:, :], in_=gcol_d), [], [GK], dma=True)
        for l in range(N_A):
            for hf in range(2):
                tmp, tk = SM.get()
                add("sp", (lambda g, tmp=tmp, l=l, hf=hf: g.dma_start(
                    out=tmp, in_=wsT_d[l * 128:(l + 1) * 128, hf * 512:(hf + 1) * 512])), [], [tk], dma=True)
                for g4 in range(4):
                    o = l * 1024 + hf * 512 + g4 * 128
                    tt("dve", WSTt[:, o:o + 128], tmp[:, g4 * 128:(g4 + 1) * 128], TRI[:, :], ALU.mult,
                       [tk, CK], [("WST", l)])
        lamt, lk = SM.get()
        add("sp", lambda g: g.dma_start(out=lamt, in_=lamrep_d), [], [lk], dma=True)
        prod, pk = SM.get()
        for j in range(2):
            layer = N_A + j
            lam_init = 0.8 - 0.6 * math.exp(-0.3 * layer)
            for c in range(2):
                a0 = (j * 4 + 2 * c) * 64
                tt("dve", prod[:, 0:64], lamt[:, a0:a0 + 64], lamt[:, a0 + 64:a0 + 128], ALU.mult, [lk], [pk])
                sc_ap = STt[:, 8 * 64 - 8 + c:8 * 64 - 8 + c + 1]
                add("dve", (lambda e, sc_ap=sc_ap: e.reduce_sum(out=sc_ap, in_=prod[:, 0:64],
                                                                axis=mybir.AxisListType.X)), [pk], [("LS", c)])
                actf(sc_ap, sc_ap, AF.Exp, [("LS", c)], [("LS", c)])
            e1 = STt[:, 8 * 64 - 8:8 * 64 - 7]
            e2 = STt[:, 8 * 64 - 7:8 * 64 - 6]
            stt("dve", gc(C_NLAM + j), e2, -lam_init, e1, ALU.add, ALU.subtract,
                [("LS", 0), ("LS", 1), GK], [GK])
            ts("dve", gc(C_SUBS + j), gc(C_SUB + j), 1.0 - lam_init, None, ALU.mult, None, [GK], [GK])

        STATB = (6, 7)
        prestat = {"pend": None, "cnt": [0, 0]}

        def flush_stat():
            p = prestat["pend"]
            if p is not None:
                sq, sk, sub = p
                c = prestat["cnt"][sub]
                mm(PS(STATB[sub]), ONESD[:, :], sq, c == 0, c == 7, [sk, CK], [psk(STATB[sub])])
                prestat["cnt"][sub] = c + 1
                prestat["pend"] = None

        def h_updated(dc, sub):
            flush_stat()
            h, hk = H(dc, sub)
            sq, sk = SQ.get()
            tt("dve", sq, h, h, ALU.mult, [hk], [sk])
            prestat["pend"] = (sq, sk, sub)

        def emit_norm(cbase, final_tile=None):
            flush_stat()
            pre = prestat["cnt"] == [8, 8]
            assert pre or prestat["cnt"] == [0, 0]
            prestat["cnt"] = [0, 0]
            banks = []
            for sub in range(2):
                if pre:
                    banks.append(STATB[sub])
                    continue
                b = psalloc()
                banks.append(b)
                eng = "dve"
                for kc in range(8):
                    h, hk = H(kc, sub)
                    sq, sk = SQ.get()
                    tt(eng, sq, h, h, ALU.mult, [hk], [sk])
                    mm(PS(b), ONESD[:, :], sq, kc == 0, kc == 7, [sk, CK], [psk(b)])
            rstds = []
            for sub in range(2):
                rstd, rk = RS.get(width=SUB)
                rsqrt_act(rstd, PS(banks[sub]), EPSC[:, 0:1], [psk(banks[sub]), CK], [rk])
                rstds.append((rstd, rk))
            for sub in range(2):
                rstd, rk = rstds[sub]
                eng = "dve"
                for kc in range(8):
                    h, hk = H(kc, sub)
                    if final_tile is None:
                        x, xk = XN(kc, sub)
                        stt(eng, x, h, gc(cbase + kc), rstd, ALU.mult, ALU.mult, [hk, rk, GK], [xk])
                    else:
                        o, ok = SM.get()
                        stt("dve", o, h, gc(cbase + kc), rstd, ALU.mult, ALU.mult, [hk, rk, GK], [ok])
                        t0 = final_tile * TT + sub * SUB
                        add("sp", (lambda g, o=o, kc=kc, t0=t0: g.dma_start(
                            out=outT[kc * 128:(kc + 1) * 128, t0:t0 + SUB], in_=o)), [ok], [], dma=True)

        def emit_ffn(l, j):
            emit_norm(C_NORM + (l * 3 + (0 if j == 0 else 2)) * 8)
            for b in range(11):
                base, wk = wstr.acquire()
                for fc in range(2):
                    f = b * 2 + fc
                    for sub in range(2):
                        pg = psalloc()
                        pu = psalloc()
                        for k in range(8):
                            x, xk = XN(k, sub)
                            o = base + k * 512 + fc * 128
                            mm(PS(pg), WSt[:, o:o + 128], x, k == 0, k == 7, wk + [xk], [psk(pg)])
                        for k in range(8):
                            x, xk = XN(k, sub)
                            o = base + k * 512 + 256 + fc * 128
                            mm(PS(pu), WSt[:, o:o + 128], x, k == 0, k == 7, wk + [xk], [psk(pu)])
                        sg, sgk = SM.get()
                        actf(sg, PS(pg), AF.Silu, [psk(pg)], [sgk])
                        a, ak = BIG(f * TT + sub * SUB, SUB)
                        tt("dve", a, sg, PS(pu), ALU.mult, [sgk, psk(pu)], ak)
            for b in range(4):
                base, wk = wstr.acquire()
                for dc2 in range(2):
                    dc = b * 2 + dc2
                    for sub in range(2):
                        p = psalloc()
                        for f in range(22):
                            a, ak = BIG(f * TT + sub * SUB, SUB)
                            o = base + f * 256 + dc2 * 128
                            mm(PS(p), WSt[:, o:o + 128], a, f == 0, f == 21, wk + ak, [psk(p)])
                        h, hk = H(dc, sub)
                        stt("dve", h, PS(p), 0.5, h, ALU.mult, ALU.add, [psk(p), hk], [hk])
                        h_updated(dc, sub)

        VH0 = 0
        GT0 = 12288

        def emit_amix(l):
            emit_norm(C_NORM + (l * 3 + 1) * 8)
            add("sp", (lambda g: g.dma_start(out=BSt[:, :], in_=bsrep_d[l * 128:(l + 1) * 128, :])),
                [], [("BS",)], dma=True)
            for hf in range(2):
                b = psalloc()
                o = l * 1024 + hf * 512
                mm(PS(b), ONES1[:, :], WSTt[:, o:o + 512], True, True, [("WST", l), CK], [psk(b)])
                for g4 in range(4):
                    g = hf * 4 + g4
                    for c3 in range(3):
                        cc = g * 3 + c3
                        stt("dve", B2t[:, cc * 128:(cc + 1) * 128], PS(b, g4 * 128, 128), gc(C_LNB + l * 24 + cc),
                            BSt[:, g * 128:(g + 1) * 128], ALU.mult, ALU.add,
                            [psk(b), GK, ("BS",)], [("B2", cc)])
            for sub in range(2):
                stats = []
                for tc in range(4):
                    stats.append(ST.get())
                for vb in range(6):
                    base, wk = wstr.acquire()
                    for tc in range(4):
                        b = psalloc()
                        for k in range(8):
                            o = k * TT + sub * SUB + tc * 128
                            mm(PS(b), XNt[:, o:o + 128], WSt[:, base + k * 512:base + (k + 1) * 512],
                               k == 0, False, wk + [("XN", k, sub)], [psk(b)])
                        mm(PS(b), ONES1[0:1, :], WSt[0:1, base + 4096:base + 4608], False, True,
                           wk + [CK], [psk(b)])
                        v, vk = BIG(VH0 + tc * DG + vb * 512, 512)
                        actf(v, PS(b), AF.Gelu, [psk(b)], vk)
                        sa, sk = stats[tc]
                        add("dve", (lambda e, sa=sa, v=v, vb=vb: e.bn_stats(out=sa[:, vb * 6:(vb + 1) * 6], in_=v)),
                            vk, [sk])
                for tc in range(4):
                    sa, sk = stats[tc]
                    add("dve", (lambda e, sa=sa: e.bn_aggr(out=sa[:, 36:38], in_=sa[:, 0:36])), [sk], [sk])
                    rsqrt_act(sa[:, 38:39], sa[:, 37:38], EPSC[:, 1:2], [sk, CK], [sk])
                    ts("dve", sa[:, 39:40], sa[:, 36:37], sa[:, 38:39], -1.0, ALU.mult, ALU.mult, [sk], [sk])
                    v, vk = BIG(VH0 + tc * DG, DG)
                    ts("dve", v, v, sa[:, 38:39], sa[:, 39:40], ALU.mult, ALU.add, [sk] + vk, vk)
                for ub in range(6):
                    base, wk = wstr.acquire()
                    for c4 in range(4):
                        cc = ub * 4 + c4
                        g = cc // 3
                        pu = psalloc()
                        for k in range(8):
                            x, xk = XN(k, sub)
                            o = base + k * 512 + c4 * 128
                            mm(PS(pu), WSt[:, o:o + 128], x, k == 0, k == 7, wk + [xk], [psk(pu)])
                        u, uk = SQ.get()
                        actf(u, PS(pu), AF.Gelu, [psk(pu), GK], [uk], bias=gc(C_BU + l * 24 + cc))
                        pss = psalloc()
                        for tc in range(4):
                            vv, vk = BIG(VH0 + tc * DG + cc * 128, 128)
                            wo_ = l * 1024 + g * 128
                            mm(PS(pss, tc * 128, 128), vv, WSTt[:, wo_:wo_ + 128], True, True,
                               vk + [("WST", l)], [psk(pss)])
                        tmp, tk = SM.get()
                        b2 = B2t[:, cc * 128:(cc + 1) * 128].unsqueeze(1).broadcast_to([128, 4, 128])
                        stt("dve", tmp.rearrange("p (a t) -> p a t", a=4),
                            PS(pss).rearrange("p (a t) -> p a t", a=4),
                            gc(C_LNG + l * 24 + cc), b2, ALU.mult, ALU.add,
                            [psk(pss), GK, ("B2", cc)], [tk])
                        gt, gk = BIG(GT0 + cc * 512, 512)
                        tt("dve", gt, tmp, u, ALU.mult, [tk, uk], gk)
                for b4 in range(4):
                    base, wk = wstr.acquire()
                    for dc2 in range(2):
                        dc = b4 * 2 + dc2
                        p = psalloc()
                        for c in range(24):
                            gt, gk = BIG(GT0 + c * 512, 512)
                            o = base + c * 256 + dc2 * 128
                            mm(PS(p), WSt[:, o:o + 128], gt, c == 0, c == 23, wk + gk, [psk(p)])
                        h, hk = H(dc, sub)
                        stt("dve", h, PS(p), gc(C_BOUT + l * 8 + dc), h, ALU.add, ALU.add,
                            [psk(p), hk, GK], [hk])
                        h_updated(dc, sub)
                flush_stat()

        def emit_kv(tile):
            emit_norm(C_KVN)
            for b in range(2):
                base, wk = wstr.acquire()
                for c4 in range(4):
                    hd = b * 4 + c4
                    for sub in range(2):
                        p = psalloc()
                        for k in range(8):
                            x, xk = XN(k, sub)
                            o = base + k * 512 + c4 * 128
                            mm(PS(p), WSt[:, o:o + 128], x, k == 0, k == 7, wk + [xk], [psk(p)])
                        stg, sk = BIG((hd * 2 + sub) * 512, 512)
                        add("dve", (lambda e, stg=stg, p=p: e.tensor_copy(out=stg, in_=PS(p))), [psk(p)], sk)
                        t0 = tile * TT + sub * SUB
                        add("sp", (lambda g, stg=stg, hd=hd, t0=t0: g.dma_start(
                            out=kT_d[hd * 128:(hd + 1) * 128, t0:t0 + SUB], in_=stg)),
                            sk, [("kT", tile)], dma=True)
            for b in range(2):
                base, wk = wstr.acquire()
                for tc in range(8):
                    sub = tc // 4
                    p = psalloc()
                    for k in range(8):
                        o = k * TT + tc * 128
                        mm(PS(p), XNt[:, o:o + 128], WSt[:, base + k * 512:base + (k + 1) * 512],
                           k == 0, k == 7, wk + [("XN", k, sub)], [psk(p)])
                    stg, sk = BIG(8192 + (b * 8 + tc) * 512, 512)
                    add("dve", (lambda e, stg=stg, p=p: e.tensor_copy(out=stg, in_=PS(p))), [psk(p)], sk)
                    ch = tile * 8 + tc
                    add("sp", (lambda g, stg=stg, b=b, ch=ch: g.dma_start(
                        out=vS_d[b * 512:(b + 1) * 512, ch * 128:(ch + 1) * 128].rearrange("(h p) e -> p h e", p=128),
                        in_=stg.rearrange("p (h e) -> p h e", h=4))),
                        sk, [("vS", tile)], dma=True)

        QT0 = 0
        KB0 = 8192
        VB0 = 16384
        SBANKS = (0, 1, 2)
        MISC = 3
        ACC = (4, 5, 6, 7)

        def emit_attn(j, l, tile):
            emit_norm(C_NORM + (l * 3 + 1) * 8)
            for b in range(2):
                base, wk = wstr.acquire()
                for c4 in range(4):
                    hd = b * 4 + c4
                    for sub in range(2):
                        p = psalloc(SBANKS + (MISC,))
                        for k in range(8):
                            x, xk = XN(k, sub)
                            o = base + k * 512 + c4 * 128
                            mm(PS(p), WSt[:, o:o + 128], x, k == 0, k == 7, wk + [xk], [psk(p)])
                        q, qk = BIG(QT0 + hd * TT + sub * SUB, SUB)
                        ts("dve", q, PS(p), 0.125, None, ALU.mult, None, [psk(p)], qk)
            kend = (tile + 1) * TT
            nkc = kend // 128
            sstate = {"i": 0}

            def load_head(hd):
                bi = hd % 2
                kb, kk = BIG(KB0 + bi * 4096, kend)
                vb, vk = BIG(VB0 + bi * 4096, kend)
                rk = [("kT", t) for t in range(tile + 1)]
                rv = [("vS", t) for t in range(tile + 1)]
                add("sp", (lambda g: g.dma_start(out=kb, in_=kT_d[hd * 128:(hd + 1) * 128, 0:kend])),
                    rk, kk, dma=True)
                add("sp", (lambda g: g.dma_start(out=vb, in_=vS_d[hd * 128:(hd + 1) * 128, 0:kend])),
                    rv, vk, dma=True)

            load_head(0)
            pending = [None]
            pending1 = [None]
            DEFER_K = 3
            for hd in range(NH):
                if hd + 1 < NH:
                    load_head(hd + 1)
                bi = hd % 2
                for sub in range(2):
                    q0 = tile * TT + sub * SUB
                    nk = (q0 + SUB) // 128
                    pt = {}

                    def emit_S(kc):
                        off = max(0, kc * 128 - q0)
                        n = SUB - off
                        slot = sstate["i"] % 2
                        sstate["i"] += 1
                        ko = KB0 + bi * 4096 + kc * 128
                        qo = QT0 + hd * TT + sub * SUB + off
                        for c in range(2):
                            mm(PS(2 * slot + c, off, n), BIGt[c * 64:(c + 1) * 64, ko:ko + 128],
                               BIGt[c * 64:(c + 1) * 64, qo:qo + n], True, True,
                               BIG(ko, 128)[1] + BIG(qo, n)[1], [psk(2 * slot + c)])
                        p, pk = PT.get()
                        p3 = p.rearrange("p (c n) -> p c n", c=2)
                        s3 = PSALL[:, slot * 1024:(slot + 1) * 1024].rearrange("p (c n) -> p c n", c=2)
                        if off == 0:
                            actf(p, PSALL[:, slot * 1024:(slot + 1) * 1024], AF.Exp,
                                 [psk(2 * slot), psk(2 * slot + 1)], [pk])
                        else:
                            actf(p3[:, :, off:SUB], s3[:, :, off:SUB], AF.Exp,
                                 [psk(2 * slot), psk(2 * slot + 1)], [pk])
                        if kc * 128 >= q0:
                            tt("dve", p3[:, :, off:off + 128], p3[:, :, off:off + 128],
                               TRI[:, :].unsqueeze(1).broadcast_to([128, 2, 128]), ALU.mult, [pk, CK], [pk])
                        pt[kc] = (p, pk, off)

                    def emit_A(kc):
                        p, pk, off = pt.pop(kc)
                        n = SUB - off
                        vo = VB0 + bi * 4096 + kc * 128
                        first = (kc == 0)
                        last = (kc == nk - 1)
                        for c in range(2):
                            mm(PS(ACC[c], off, n), BIGt[:, vo:vo + 128], p[:, c * SUB + off:(c + 1) * SUB],
                               first, last, BIG(vo, 128)[1] + [pk], [psk(ACC[c])])
                        for c in range(2):
                            mm(PS(ACC[2 + c], off, n), ONES1[:, :], p[:, c * SUB + off:(c + 1) * SUB],
                               first, last, [CK, pk], [psk(ACC[2 + c])])

                    LA = 2
                    for kc in range(min(LA, nk)):
                        emit_S(kc)
                    if pending1[0] is not None:
                        pending1[0]()
                        pending1[0] = None
                    for kc in range(nk):
                        if kc + LA < nk:
                            emit_S(kc + LA)
                        emit_A(kc)
                        if pending[0] is not None and kc == min(nk - 1, DEFER_K):
                            pending[0]()
                            pending[0] = None
                    rs2, rs2k = RS.get()
                    actf(rs2, PSALL[:, ACC[2] * 512:(ACC[3] + 1) * 512], AF.Ln,
                         [psk(ACC[2]), psk(ACC[3])], [rs2k])
                    t12, t12k = OC.get()
                    add("dve", (lambda e, t12=t12: e.tensor_copy(
                        out=t12, in_=PSALL[:, ACC[0] * 512:(ACC[1] + 1) * 512])),
                        [psk(ACC[0]), psk(ACC[1])], [t12k])
                    osq, osk = SQ.get()
                    o, okk = t12[:, 0:SUB], t12k

                    def part1b(rs2=rs2, rs2k=rs2k, t12=t12, t12k=t12k, osq=osq, osk=osk):
                        actf(rs2, rs2, AF.Exp, [rs2k], [rs2k], scale=-1.0)
                        tt("dve", t12, t12, rs2, ALU.mult, [t12k, rs2k], [t12k])
                        stt("dve", t12[:, 0:SUB], t12[:, SUB:2 * SUB], gc(C_NLAM + j), t12[:, 0:SUB],
                            ALU.mult, ALU.add, [t12k, GK], [t12k])
                        tt("dve", osq, t12[:, 0:SUB], t12[:, 0:SUB], ALU.mult, [t12k], [osk])

                    pending1[0] = part1b

                    def part2(o=o, okk=okk, osq=osq, osk=osk, hd=hd, sub=sub):
                        slot = sstate["i"] % 2
                        sstate["i"] += 1
                        mm(PS(2 * slot), ONESV[:, :], osq, True, True, [osk, CK], [psk(2 * slot)])
                        rs, rsk = SM.get()
                        rsqrt_act(rs, PS(2 * slot), EPSC[:, 0:1], [psk(2 * slot), CK], [rsk])
                        on, onk = XN(hd, sub)
                        stt("dve", on, o, gc(C_SUBS + j), rs, ALU.mult, ALU.mult, [okk, rsk, GK], [onk])

                    pending[0] = part2
            if pending1[0] is not None:
                pending1[0]()
                pending1[0] = None
            if pending[0] is not None:
                pending[0]()
                pending[0] = None
            r0 = j * D
            for b in range(2):
                base, wk = wstr.acquire()
                for c4 in range(4):
                    dc = b * 4 + c4
                    for sub in range(2):
                        p = psalloc(SBANKS + (MISC,))
                        for hd in range(NH):
                            on, onk = XN(hd, sub)
                            o = base + hd * 512 + c4 * 128
                            mm(PS(p), WSt[:, o:o + 128], on, hd == 0, hd == NH - 1, wk + [onk], [psk(p)])
                        h, hk = H(dc, sub)
                        tt("dve", h, PS(p), h, ALU.add, [psk(p), hk], [hk])
                        h_updated(dc, sub)

        for t in range(NT):
            for sub in range(2):
                for kc in range(8):
                    h, hk = H(kc, sub)
                    t0 = t * TT + sub * SUB
                    add("sp", (lambda g, h=h, kc=kc, t0=t0: g.dma_start(
                        out=h, in_=xT[kc * 128:(kc + 1) * 128, t0:t0 + SUB])), [], [hk], dma=True)
            flush_stat()
            prestat["cnt"] = [0, 0]
            for st in plan:
                if st[0] == "ffn":
                    emit_ffn(st[1], st[2])
                elif st[0] == "amix":
                    emit_amix(st[1])
                elif st[0] == "kv":
                    emit_kv(t)
                elif st[0] == "attn":
                    emit_attn(st[1], st[2], t)
                elif st[0] == "final":
                    emit_norm(C_FIN, final_tile=t)
            if not has_final:
                for kc in range(8):
                    for sub in range(2):
                        h, hk = H(kc, sub)
                        t0 = t * TT + sub * SUB
                        add("sp", (lambda g, h=h, kc=kc, t0=t0: g.dma_start(
                            out=outT[kc * 128:(kc + 1) * 128, t0:t0 + SUB], in_=h)), [hk], [], dma=True)

        sc.resolve()
        with nc.Block() as block:
            @block.tensor
            def _(e):
                sc.run_engine("pe", e, sems)

            @block.scalar
            def _(e):
                sc.run_engine("act", e, sems)

            @block.vector
            def _(e):
                sc.run_engine("dve", e, sems)

            @block.gpsimd
            def _(e):
                sc.run_engine("pool", e, sems, final_wait=True)

            @block.sync
            def _(e):
                sc.run_engine("sp", e, sems, final_wait=True)
    return nc


def make_shared_inputs(norm_g, ffn_w_gate_up, ffn_w_down, a_w_in, a_b_in, a_ln_g, a_ln_b,
                       a_w_s, a_b_s, a_w_out, a_b_out, kv_norm_g, w_kv, b_w_q, b_lambda,
                       b_subln_g, b_w_o, final_norm_g):
    f = np.float32
    gcol = np.zeros((128, NCOL), f)
    gcol[:, C_NORM:C_NORM + 96] = np.asarray(norm_g, f).reshape(12, 8, 128).transpose(2, 0, 1).reshape(128, 96)
    gcol[:, C_KVN:C_KVN + 8] = np.asarray(kv_norm_g, f).reshape(8, 128).T
    gcol[:, C_FIN:C_FIN + 8] = np.asarray(final_norm_g, f).reshape(8, 128).T
    abin = np.asarray(a_b_in, f)
    gcol[:, C_BU:C_BU + 48] = abin[:, :DG].reshape(2, 24, 128).transpose(2, 0, 1).reshape(128, 48)
    gcol[:, C_LNG:C_LNG + 48] = np.asarray(a_ln_g, f).reshape(2, 24, 128).transpose(2, 0, 1).reshape(128, 48)
    gcol[:, C_LNB:C_LNB + 48] = np.asarray(a_ln_b, f).reshape(2, 24, 128).transpose(2, 0, 1).reshape(128, 48)
    gcol[:, C_BOUT:C_BOUT + 16] = np.asarray(a_b_out, f).reshape(2, 8, 128).transpose(2, 0, 1).reshape(128, 16)
    gcol[:, C_SUB:C_SUB + 2] = np.asarray(b_subln_g, f).T
    tri = np.triu(np.ones((128, 128), f))
    ws = np.asarray(a_w_s, f)
    wsT = np.ascontiguousarray(ws.transpose(0, 3, 1, 2)).reshape(N_A * 128, 1024)
    bs = np.asarray(a_b_s, f).reshape(N_A, 1, 1024)
    bsrep = np.ascontiguousarray(np.broadcast_to(bs, (N_A, 128, 1024))).reshape(N_A * 128, 1024)
    lamrep = np.ascontiguousarray(np.broadcast_to(np.asarray(b_lambda, f).reshape(1, 512), (128, 512)))
    return {
        "gcol": gcol,
        "tri": tri,
        "wgu": np.ascontiguousarray(np.asarray(ffn_w_gate_up, f)).reshape(DEPTH * 2 * D, 2 * DFF),
        "wdn": np.ascontiguousarray(np.asarray(ffn_w_down, f)).reshape(DEPTH * 2 * DFF, D),
        "win": np.ascontiguousarray(np.asarray(a_w_in, f)).reshape(N_A * D, 2 * DG),
        "bin": np.ascontiguousarray(abin),
        "wsT": wsT,
        "bsrep": bsrep,
        "wout": np.ascontiguousarray(np.asarray(a_w_out, f)).reshape(N_A * DG, D),
        "wkv": np.ascontiguousarray(np.asarray(w_kv, f)),
        "wq": np.ascontiguousarray(np.asarray(b_w_q, f)).reshape(2 * D, D),
        "wo": np.ascontiguousarray(np.asarray(b_w_o, f)).reshape(2 * D, D),
        "lamrep": lamrep,
    }


_NC_CACHE = {}


def kernel(x, norm_g, ffn_w_gate_up, ffn_w_down, a_w_in, a_b_in, a_ln_g, a_ln_b,
           a_w_s, a_b_s, a_w_out, a_b_out, kv_norm_g, w_kv, b_w_q, b_lambda,
           b_subln_g, b_w_o, final_norm_g):
    x = np.asarray(x, np.float32)
    B, S, _ = x.shape
    shared = make_shared_inputs(norm_g, ffn_w_gate_up, ffn_w_down, a_w_in, a_b_in, a_ln_g, a_ln_b,
                                a_w_s, a_b_s, a_w_out, a_b_out, kv_norm_g, w_kv, b_w_q, b_lambda,
                                b_subln_g, b_w_o, final_norm_g)
    key = ("full", S)
    if key not in _NC_CACHE:
        _NC_CACHE[key] = build_program(S=S)
    nc = _NC_CACHE[key]
    in_maps = []
    for b in range(B):
        m = dict(shared)
        m["xT"] = np.ascontiguousarray(x[b].T)
        in_maps.append(m)
    res = run_bass_kernel_spmd(nc, in_maps, core_ids=list(range(B)))
    out = np.empty((B, S, D), np.float32)
    for b in range(B):
        out[b] = res.results[b]["outT"].T
    return out
```
